# Optimizing a Trainium2 kernel written in Bass

```python
import math
import jax, jax.numpy as jnp
from jax import lax
import numpy as np

D_MODEL = 2048
BATCH = 4
SEQ = 4096
DEPTH = 2

HEAD_DIM = 128
N_SB_HEADS = D_MODEL // (2 * HEAD_DIM)
N_DIFF_HEADS = D_MODEL // (4 * HEAD_DIM)
SB_WIDTH = N_SB_HEADS * HEAD_DIM
DIFF_WIDTH = N_DIFF_HEADS * 2 * HEAD_DIM
MIX_WIDTH = SB_WIDTH + DIFF_WIDTH
IN_WIDTH = 3 * SB_WIDTH + 3 * DIFF_WIDTH
D_FF = 4 * D_MODEL
N_BUCKETS = 32
MAX_DISTANCE = 128
Q_BLOCK = 128
LN_EPS = 1e-5
RMS_EPS = 1e-5
NEG_BIG = -1e30
ALPHA = (2 * DEPTH) ** 0.25
INIT_BETA = (8 * DEPTH) ** -0.25

kernel_name = "hybrid_sb_diff_attn_deepnorm"


def layernorm(x, g, b):
    xf = x.astype(jnp.float32)
    mu = jnp.mean(xf, axis=-1, keepdims=True)
    var = jnp.mean(jnp.square(xf - mu), axis=-1, keepdims=True)
    y = (xf - mu) * lax.rsqrt(var + LN_EPS) * g.astype(jnp.float32) + b.astype(jnp.float32)
    return y.astype(x.dtype)


def rmsnorm(x, g):
    xf = x.astype(jnp.float32)
    y = xf * lax.rsqrt(jnp.mean(jnp.square(xf), axis=-1, keepdims=True) + RMS_EPS)
    return y * g.astype(jnp.float32)


def t5_causal_bucket(dist):
    n = jnp.maximum(dist, 0)
    max_exact = N_BUCKETS // 2
    nf = jnp.maximum(n, 1).astype(jnp.float32)
    large = max_exact + (jnp.log(nf / max_exact) / math.log(MAX_DISTANCE / max_exact)
                         * (N_BUCKETS - max_exact)).astype(jnp.int32)
    large = jnp.minimum(large, N_BUCKETS - 1)
    return jnp.where(n < max_exact, n, large)


def stick_breaking_attention(q, k, v):
    B, H, S, d = q.shape
    nb = S // Q_BLOCK
    scale = 1.0 / math.sqrt(d)
    kf = k.astype(jnp.float32)
    vf = v.astype(jnp.float32)
    q_blocks = jnp.moveaxis(q.reshape(B, H, nb, Q_BLOCK, d), 2, 0)
    s_pos = jnp.arange(S)

    def block(args):
        i, qb = args
        z = jnp.einsum('bhqd,bhkd->bhqk', qb.astype(jnp.float32), kf) * scale
        t_pos = i * Q_BLOCK + jnp.arange(Q_BLOCK)
        causal = s_pos[None, :] < t_pos[:, None]
        log_fail = jnp.where(causal, jax.nn.log_sigmoid(-z), 0.0)
        suffix = lax.cumsum(log_fail, axis=3, reverse=True) - log_fail
        w = jnp.where(causal, jnp.exp(jax.nn.log_sigmoid(z) + suffix), 0.0)
        return jnp.einsum('bhqk,bhkd->bhqd', w, vf)

    out = lax.map(block, (jnp.arange(nb), q_blocks))
    out = jnp.moveaxis(out, 0, 2).reshape(B, H, S, d)
    return jnp.transpose(out, (0, 2, 1, 3))


def differential_attention(q, k, v, lam, rel_bias):
    B, H, _, S, d = q.shape
    nb = S // Q_BLOCK
    scale = 1.0 / math.sqrt(d)
    kf = k.astype(jnp.float32)
    vf = v.astype(jnp.float32)
    q_blocks = jnp.moveaxis(q.reshape(B, H, 2, nb, Q_BLOCK, d), 3, 0)
    s_pos = jnp.arange(S)
    table = rel_bias.astype(jnp.float32)

    def block(args):
        i, qb = args
        t_pos = i * Q_BLOCK + jnp.arange(Q_BLOCK)
        dist = t_pos[:, None] - s_pos[None, :]
        bias = jnp.transpose(table[t5_causal_bucket(dist)], (2, 0, 1))
        logits = jnp.einsum('bhmqd,bhmkd->bhmqk', qb.astype(jnp.float32), kf) * scale
        logits = logits + bias[None, :, None]
        logits = jnp.where((dist >= 0)[None, None, None], logits, NEG_BIG)
        p = jax.nn.softmax(logits, axis=-1)
        a = p[:, :, 0] - lam * p[:, :, 1]
        return jnp.einsum('bhqk,bhkd->bhqd', a, vf)

    out = lax.map(block, (jnp.arange(nb), q_blocks))
    out = jnp.moveaxis(out, 0, 2).reshape(B, H, S, 2 * d)
    return jnp.transpose(out, (0, 2, 1, 3))


def hybrid_mixer(h, w_in, w_out, sb_norm_g, lam_q1, lam_k1, lam_q2, lam_k2, diff_norm_g,
                 rel_bias, layer_idx):
    B, S, _ = h.shape
    proj = jnp.einsum('bsd,de->bse', h, w_in)
    cuts = list(np.cumsum([SB_WIDTH] * 3 + [DIFF_WIDTH] * 2))
    sb_q, sb_k, sb_v, df_q, df_k, df_v = jnp.split(proj, cuts, axis=-1)

    def sb_heads(t):
        return jnp.transpose(t.reshape(B, S, N_SB_HEADS, HEAD_DIM), (0, 2, 1, 3))
    sb_out = stick_breaking_attention(sb_heads(sb_q), sb_heads(sb_k), sb_heads(sb_v))
    sb_out = rmsnorm(sb_out, sb_norm_g).reshape(B, S, SB_WIDTH)

    def qk_heads(t):
        return jnp.transpose(t.reshape(B, S, N_DIFF_HEADS, 2, HEAD_DIM), (0, 2, 3, 1, 4))
    v_d = jnp.transpose(df_v.reshape(B, S, N_DIFF_HEADS, 2 * HEAD_DIM), (0, 2, 1, 3))
    lam_init = 0.8 - 0.6 * math.exp(-0.3 * layer_idx)
    lam = (jnp.exp(jnp.sum(lam_q1.astype(jnp.float32) * lam_k1.astype(jnp.float32)))
           - jnp.exp(jnp.sum(lam_q2.astype(jnp.float32) * lam_k2.astype(jnp.float32)))
           + lam_init)
    df_out = differential_attention(qk_heads(df_q), qk_heads(df_k), v_d, lam, rel_bias)
    df_out = (rmsnorm(df_out, diff_norm_g) * (1.0 - lam_init)).reshape(B, S, DIFF_WIDTH)

    merged = jnp.concatenate([sb_out, df_out], axis=-1).astype(h.dtype)
    return jnp.einsum('bse,ed->bsd', merged, w_out)


def squared_relu_mlp(h, w_up, w_down):
    u = jnp.einsum('bsd,df->bsf', h, w_up)
    return jnp.einsum('bsf,fd->bsd', jnp.square(jax.nn.relu(u)), w_down)


def setup_inputs(seed: int = 0) -> dict:
    key = jax.random.key(seed)
    ks = jax.random.split(key, 20)
    f32 = jnp.float32

    def nrm(k, shape, scale):
        return jax.random.normal(k, shape, f32) * scale

    x = jax.random.normal(ks[0], (BATCH, SEQ, D_MODEL), f32)
    ln0_g = 1.0 + nrm(ks[1], (D_MODEL,), 0.02)
    ln0_b = nrm(ks[2], (D_MODEL,), 0.02)
    col_scale = np.concatenate([
        np.ones(2 * SB_WIDTH), np.full(SB_WIDTH, INIT_BETA),
        np.ones(2 * DIFF_WIDTH), np.full(DIFF_WIDTH, INIT_BETA)]).astype(np.float32)
    w_in = nrm(ks[3], (DEPTH, D_MODEL, IN_WIDTH), D_MODEL ** -0.5) * jnp.asarray(col_scale)
    w_out = nrm(ks[4], (DEPTH, MIX_WIDTH, D_MODEL), MIX_WIDTH ** -0.5 * INIT_BETA)
    sb_norm_g = 1.0 + nrm(ks[5], (DEPTH, HEAD_DIM), 0.02)
    lam_q1 = nrm(ks[6], (DEPTH, HEAD_DIM), 0.1)
    lam_k1 = nrm(ks[7], (DEPTH, HEAD_DIM), 0.1)
    lam_q2 = nrm(ks[8], (DEPTH, HEAD_DIM), 0.1)
    lam_k2 = nrm(ks[9], (DEPTH, HEAD_DIM), 0.1)
    diff_norm_g = 1.0 + nrm(ks[10], (DEPTH, 2 * HEAD_DIM), 0.02)
    rel_bias = nrm(ks[11], (N_BUCKETS, N_DIFF_HEADS), 0.5)
    ln1_g = 1.0 + nrm(ks[12], (DEPTH, D_MODEL), 0.02)
    ln1_b = nrm(ks[13], (DEPTH, D_MODEL), 0.02)
    w_up = nrm(ks[14], (DEPTH, D_MODEL, D_FF), D_MODEL ** -0.5 * INIT_BETA)
    w_down = nrm(ks[15], (DEPTH, D_FF, D_MODEL), D_FF ** -0.5 * INIT_BETA)
    ln2_g = 1.0 + nrm(ks[16], (DEPTH, D_MODEL), 0.02)
    ln2_b = nrm(ks[17], (DEPTH, D_MODEL), 0.02)
    return {"x": x, "ln0_g": ln0_g, "ln0_b": ln0_b, "w_in": w_in, "w_out": w_out,
            "sb_norm_g": sb_norm_g, "lam_q1": lam_q1, "lam_k1": lam_k1,
            "lam_q2": lam_q2, "lam_k2": lam_k2, "diff_norm_g": diff_norm_g,
            "rel_bias": rel_bias, "ln1_g": ln1_g, "ln1_b": ln1_b,
            "w_up": w_up, "w_down": w_down, "ln2_g": ln2_g, "ln2_b": ln2_b}


def reference(x, ln0_g, ln0_b, w_in, w_out, sb_norm_g, lam_q1, lam_k1, lam_q2, lam_k2,
              diff_norm_g, rel_bias, ln1_g, ln1_b, w_up, w_down, ln2_g, ln2_b):
    h = layernorm(x, ln0_g, ln0_b)
    for l in range(DEPTH):
        mix = hybrid_mixer(h, w_in[l], w_out[l], sb_norm_g[l], lam_q1[l], lam_k1[l],
                           lam_q2[l], lam_k2[l], diff_norm_g[l], rel_bias, l)
        h = layernorm(ALPHA * h + mix, ln1_g[l], ln1_b[l])
        ff = squared_relu_mlp(h, w_up[l], w_down[l])
        h = layernorm(ALPHA * h + ff, ln2_g[l], ln2_b[l])
    return h
```

```python
import math
from contextlib import ExitStack

import numpy as np
import ml_dtypes

import concourse.bass as bass
import concourse.mybir as mybir
from concourse.bass_utils import run_bass_kernel_spmd

F32 = mybir.dt.float32
BF16 = mybir.dt.bfloat16
AF = mybir.ActivationFunctionType
ALU = mybir.AluOpType

D = 2048
S = 4096
NB = 4
DEPTH = 2
HD = 128
DFF = 8192
INW = 6144
LN_EPS = 1e-5
RMS_EPS = 1e-5
ALPHA = (2 * DEPTH) ** 0.25
QSCALE = 1.0 / math.sqrt(HD)
NEG = -30000.0


class Sem:
    def __init__(self, nc, stack, name):
        self.h = stack.enter_context(nc.semaphore(name))
        self.n = 0


class SemRing:
    def __init__(self, nc, stack, name, n):
        self.s = [Sem(nc, stack, f"{name}{i}") for i in range(n)]
        self.n = n

    def __getitem__(self, i):
        return self.s[i % self.n]


class Prog:
    ENG = ("pe", "act", "dve", "pool", "sp")

    def __init__(self, nc):
        self.nc = nc
        self.q = {k: [] for k in self.ENG}

    def op(self, eng, fn, sem=None, inc=1):
        if sem is None:
            self.q[eng].append(fn)
            return None
        sem.n += inc
        h = sem.h
        self.q[eng].append(lambda e: fn(e).then_inc(h, inc))
        return sem.n

    def dma(self, eng, out, in_, sem):
        return self.op(eng, lambda e: e.dma_start(out=out, in_=in_), sem, 16)

    def wait(self, eng, sem, val):
        if val is None or val <= 0:
            return
        h = sem.h
        self.q[eng].append(lambda e: e.wait_ge(h, val))

    def wait_all(self, eng, ring):
        for sm in ring.s:
            self.wait(eng, sm, sm.n)

    def run(self):
        q = self.q
        if not any(q[k] for k in self.ENG):
            return
        with self.nc.Block() as block:
            @block.tensor
            def _(e):
                for f in q["pe"]:
                    f(e)

            @block.scalar
            def _(e):
                for f in q["act"]:
                    f(e)

            @block.vector
            def _(e):
                for f in q["dve"]:
                    f(e)

            @block.gpsimd
            def _(e):
                for f in q["pool"]:
                    f(e)

            @block.sync
            def _(e):
                for f in q["sp"]:
                    f(e)


def _sb(nc, st, name, shape, dt):
    return st.enter_context(nc.sbuf_tensor(name, shape, dt))


def _ps(nc, st, name, shape, dt=F32):
    return st.enter_context(nc.psum_tensor(name, shape, dt))


class MM:
    def __init__(self, P, banks, s_mm):
        self.P, self.banks, self.s_mm = P, banks, s_mm
        self.n = 0
        self.evac = {}

    def group(self, n, fn_mk):
        P = self.P
        nb = len(self.banks)
        bank = self.banks[self.n % nb]
        prev = self.n - nb
        if prev >= 0:
            sem, val = self.evac[prev]
            P.wait("pe", sem, val)
        vm = None
        for c in range(n):
            fn = fn_mk(bank, c)
            if c == n - 1:
                vm = P.op("pe", fn, self.s_mm)
            else:
                P.op("pe", fn)
        idx = self.n
        self.n += 1
        return bank, vm, idx

    def done(self, idx, sem, val):
        self.evac[idx] = (sem, val)

def emit_ln(P, xt, gt, bt, scr, s_lnA, s_lnD, done, pre_wait=None):
    stats, mv, rstd, nmr = scr["stats"], scr["mv"], scr["rstd"], scr["nmr"]
    if pre_wait is not None:
        P.wait("dve", pre_wait[0], pre_wait[1])
    for k_ in range(4):
        vs_ = P.op("dve", lambda e, k_=k_: e.bn_stats(stats[:, k_, :], xt[:, k_ * 512:(k_ + 1) * 512]), s_lnD)
    P.wait("dve", s_lnD, vs_)
    v0 = P.op("dve", lambda e: e.bn_aggr(mv[:], stats[:].rearrange("p k s -> p (k s)")), s_lnD)
    P.wait("dve", s_lnD, v0)
    P.wait("act", s_lnD, v0)
    if pre_wait is not None:
        P.wait("act", pre_wait[0], pre_wait[1])
    va = P.op("act", lambda e: e.activation(out=rstd[:], in_=mv[:, 1:2], func=AF.Ln, bias=LN_EPS, scale=1.0), s_lnA)
    P.wait("act", s_lnA, va)
    va = P.op("act", lambda e: e.activation(out=rstd[:], in_=rstd[:], func=AF.Exp, scale=-0.5), s_lnA)
    P.wait("dve", s_lnA, va)
    v2 = P.op("dve", lambda e: e.tensor_scalar(nmr[:], mv[:, 0:1], rstd[:, 0:1], -1.0, ALU.mult, ALU.mult), s_lnD)
    P.wait("act", s_lnD, v2)
    v3 = P.op("act", lambda e: e.activation(out=xt[:], in_=xt[:], func=AF.Identity, bias=nmr[:, 0:1], scale=rstd[:, 0:1]), s_lnA)
    P.wait("dve", s_lnA, v3)
    v4 = P.op("dve", lambda e: e.tensor_tensor(xt[:], xt[:], gt[:], ALU.mult), s_lnD)
    P.wait("dve", s_lnD, v4)
    v5 = P.op("dve", lambda e: e.tensor_tensor(xt[:], xt[:], bt[:], ALU.add), done)
    return v5


def phase_A(nc, xin, w_in, lng, lnb, ident_d, hres, qT, kT, v, NT, do_ln):
    ST = 2048
    NST = NT // ST
    with ExitStack() as st:
        P = Prog(nc)
        hT = _sb(nc, st, "A_hT", [128, 16, ST], BF16)
        xts = [_sb(nc, st, f"A_xt{i}", [128, D], F32) for i in range(2)]
        xbs = [_sb(nc, st, f"A_xb{i}", [128, D], BF16) for i in range(2)]
        gt = _sb(nc, st, "A_g", [128, D], F32)
        bt = _sb(nc, st, "A_b", [128, D], F32)
        ident = _sb(nc, st, "A_id", [128, 128], BF16)
        wbl = [_sb(nc, st, f"A_w{i}", [128, 16, 512], BF16) for i in range(2)]
        qst = [_sb(nc, st, f"A_qs{i}", [128, ST], BF16) for i in range(4)]
        vst = [_sb(nc, st, f"A_vs{i}", [128, 512], BF16) for i in range(4)]
        scr = {"stats": _sb(nc, st, "A_stats", [128, 4, 6], F32), "mv": _sb(nc, st, "A_mv", [128, 2], F32),
               "rstd": _sb(nc, st, "A_rstd", [128, 1], F32), "nmr": _sb(nc, st, "A_nmr", [128, 1], F32)}
        tps = [_ps(nc, st, f"A_tp{i}", [128, 1024], BF16) for i in range(2)]
        mps = [_ps(nc, st, f"A_mm{i}", [128, 512], F32) for i in range(4)]

        s_c = Sem(nc, st, "A_c")
        s_x = SemRing(nc, st, "A_x", 2)
        s_lnA = Sem(nc, st, "A_lnA")
        s_lnD = Sem(nc, st, "A_lnD")
        s_h = Sem(nc, st, "A_h")
        s_xb = Sem(nc, st, "A_xb")
        s_hst = SemRing(nc, st, "A_hst", 2)
        s_tp = Sem(nc, st, "A_tp")
        s_te = Sem(nc, st, "A_te")
        s_wl = SemRing(nc, st, "A_wl", 2)
        s_mm = Sem(nc, st, "A_mm")
        s_evA = Sem(nc, st, "A_evA")
        s_evD = Sem(nc, st, "A_evD")
        s_od = SemRing(nc, st, "A_od", 4)
        s_ov = SemRing(nc, st, "A_ov", 4)

        P.dma("sp", ident[:], ident_d, s_c)
        if do_ln:
            P.dma("sp", gt[:], lng, s_c)
            P.dma("sp", bt[:], lnb, s_c)
        c_ready = s_c.n
        for e_ in ("pe", "act", "dve"):
            P.wait(e_, s_c, c_ready)

        w_view = w_in.rearrange("(c p) e -> p c e", p=128)
        nblk = 0
        blk_end = []
        mm = MM(P, mps, s_mm)
        nq = 0
        nv = 0
        ntile = 0
        hst_vals = {}
        for sti in range(NST):
            t0 = sti * ST
            hT_free_val = s_mm.n
            for tt in range(16):
                b = ntile % 2
                xt, xb = xts[b], xbs[b]
                if ntile >= 2:
                    P.wait("sp", s_xb, ntile - 1)
                    if do_ln:
                        P.wait("sp", s_hst[b], hst_vals[ntile - 2])
                vx = P.dma("sp", xt[:], xin[t0 + tt * 128: t0 + (tt + 1) * 128, :], s_x[b])
                if do_ln:
                    vh = emit_ln(P, xt, gt, bt, scr, s_lnA, s_lnD, s_h, pre_wait=(s_x[b], vx))
                    P.wait("sp", s_h, vh)
                    hst_vals[ntile] = P.dma("sp", hres[t0 + tt * 128: t0 + (tt + 1) * 128, :], xt[:], s_hst[b])
                    P.wait("act", s_h, vh)
                else:
                    P.wait("act", s_x[b], vx)
                if ntile >= 2:
                    P.wait("act", s_tp, ntile - 1)
                vb = P.op("act", lambda e, xb=xb, xt=xt: e.activation(out=xb[:], in_=xt[:], func=AF.Copy), s_xb)
                P.wait("pe", s_xb, vb)
                P.wait("pe", s_te, 2 * ntile)
                if tt == 0:
                    P.wait("pe", s_mm, hT_free_val)
                for c in range(16):
                    tp = tps[c // 8]
                    fn = lambda e, tp=tp, c=c, xb=xb: e.transpose(tp[:, (c % 8) * 128:(c % 8 + 1) * 128], xb[:, c * 128:(c + 1) * 128], ident[:])
                    if c == 15:
                        vt = P.op("pe", fn, s_tp)
                    else:
                        P.op("pe", fn)
                P.wait("act", s_tp, vt)
                P.wait("dve", s_tp, vt)
                P.op("act", lambda e, tt=tt: e.activation(out=hT[:, 0:8, tt * 128:(tt + 1) * 128],
                                                         in_=tps[0][:].rearrange("p (c t) -> p c t", c=8), func=AF.Copy), s_te)
                P.op("dve", lambda e, tt=tt: e.tensor_copy(hT[:, 8:16, tt * 128:(tt + 1) * 128],
                                                          tps[1][:].rearrange("p (c t) -> p c t", c=8)), s_te)
                ntile += 1
            hT_ready = s_te.n
            for wb in range(12):
                wbuf = wbl[nblk % 2]
                if nblk >= 2:
                    P.wait("pool", s_mm, blk_end[nblk - 2])
                vwl = P.dma("pool", wbuf[:], w_view[:, :, wb * 512:(wb + 1) * 512], s_wl[nblk])
                P.wait("pe", s_wl[nblk], vwl)
                if wb == 0:
                    P.wait("pe", s_te, hT_ready)
                kind = ("q", "q", "k", "k", "v", "v", "q", "q", "k", "k", "v", "v")[wb]
                if kind in ("q", "k"):
                    ubase = (wb % 2) * 4 + (8 if wb >= 6 else 0)
                    for ec in range(4):
                        unit = ubase + ec
                        stg = qst[nq % 4]
                        last_ev = []
                        for tg in range(ST // 512):
                            bank, vm, idx = mm.group(16, lambda bank, c, wbuf=wbuf, ec=ec, tg=tg: (lambda e: e.matmul(
                                bank[:], wbuf[:, c, ec * 128:(ec + 1) * 128], hT[:, c, tg * 512:(tg + 1) * 512],
                                start=(c == 0), stop=(c == 15))))
                            ev = "act" if idx % 2 == 0 else "dve"
                            P.wait(ev, s_mm, vm)
                            P.wait(ev, s_od[nq], s_od[nq].n)
                            sc = QSCALE if kind == "q" else 1.0
                            if ev == "act":
                                vv = P.op("act", lambda e, stg=stg, bank=bank, tg=tg, sc=sc: e.activation(
                                    out=stg[:, tg * 512:(tg + 1) * 512], in_=bank[:], func=AF.Copy, scale=sc), s_evA)
                                mm.done(idx, s_evA, vv)
                                last_ev.append((s_evA, vv))
                            else:
                                vv = P.op("dve", lambda e, stg=stg, bank=bank, tg=tg, sc=sc: e.tensor_scalar(
                                    stg[:, tg * 512:(tg + 1) * 512], bank[:], sc, None, ALU.mult), s_evD)
                                mm.done(idx, s_evD, vv)
                                last_ev.append((s_evD, vv))
                        dst = (qT if kind == "q" else kT)[unit, :, t0:t0 + ST]
                        for sem_, val_ in last_ev[-2:]:
                            P.wait("sp", sem_, val_)
                        P.dma("sp", dst, stg[:], s_od[nq])
                        nq += 1
                else:
                    vcol0 = {4: 0, 5: 512, 10: 1024, 11: 1536}[wb]
                    for tt in range(ST // 128):
                        bank, vm, idx = mm.group(16, lambda bank, c, wbuf=wbuf, tt=tt: (lambda e: e.matmul(
                            bank[:], hT[:, c, tt * 128:(tt + 1) * 128], wbuf[:, c, :],
                            start=(c == 0), stop=(c == 15))))
                        ev = "act" if idx % 2 == 0 else "dve"
                        stg = vst[nv % 4]
                        P.wait(ev, s_mm, vm)
                        P.wait(ev, s_ov[nv], s_ov[nv].n)
                        if ev == "act":
                            vv = P.op("act", lambda e, stg=stg, bank=bank: e.activation(out=stg[:], in_=bank[:], func=AF.Copy), s_evA)
                            mm.done(idx, s_evA, vv)
                            P.wait("sp", s_evA, vv)
                        else:
                            vv = P.op("dve", lambda e, stg=stg, bank=bank: e.tensor_copy(stg[:], bank[:]), s_evD)
                            mm.done(idx, s_evD, vv)
                            P.wait("sp", s_evD, vv)
                        P.dma("sp", v[t0 + tt * 128:t0 + (tt + 1) * 128, vcol0:vcol0 + 512], stg[:], s_ov[nv])
                        nv += 1
                blk_end.append(s_mm.n)
                nblk += 1
        P.wait_all("sp", s_od)
        P.wait_all("sp", s_ov)
        if do_ln:
            P.wait_all("sp", s_hst)
        P.run()


def phase_C(nc, mT, hres, w_out, g1, b1, w_up, w_down, g2, b2, ident_d, hout, NT):
    ST = 512
    NST = NT // ST
    with ExitStack() as st:
        P = Prog(nc)
        mTs = _sb(nc, st, "C_mT", [128, 16, ST], BF16)
        acc = [_sb(nc, st, f"C_acc{i}", [128, D], F32) for i in range(4)]
        xbs = [_sb(nc, st, f"C_xb{i}", [128, D], BF16) for i in range(2)]
        h1T = _sb(nc, st, "C_h1T", [128, 16, ST], BF16)
        aT = _sb(nc, st, "C_aT", [128, 32, ST], BF16)
        rtmp = [_sb(nc, st, f"C_rt{i}", [128, 512], F32) for i in range(2)]
        wbl = [_sb(nc, st, f"C_w{i}", [128, 16 * 1024], BF16) for i in range(2)]
        g1t = _sb(nc, st, "C_g1", [128, D], F32)
        b1t = _sb(nc, st, "C_b1", [128, D], F32)
        g2t = _sb(nc, st, "C_g2", [128, D], F32)
        b2t = _sb(nc, st, "C_b2", [128, D], F32)
        ident = _sb(nc, st, "C_id", [128, 128], BF16)
        scr = {"stats": _sb(nc, st, "C_stats", [128, 4, 6], F32), "mv": _sb(nc, st, "C_mv", [128, 2], F32),
               "rstd": _sb(nc, st, "C_rstd", [128, 1], F32), "nmr": _sb(nc, st, "C_nmr", [128, 1], F32)}
        tps = [_ps(nc, st, f"C_tp{i}", [128, 1024], BF16) for i in range(2)]
        mps = [_ps(nc, st, f"C_mm{i}", [128, 512], F32) for i in range(4)]

        s_c = Sem(nc, st, "C_c")
        s_in = Sem(nc, st, "C_in")
        s_lnA = Sem(nc, st, "C_lnA")
        s_lnD = Sem(nc, st, "C_lnD")
        s_h = Sem(nc, st, "C_h")
        s_xb = Sem(nc, st, "C_xb")
        s_tp = Sem(nc, st, "C_tp")
        s_te = Sem(nc, st, "C_te")
        s_wl = SemRing(nc, st, "C_wl", 2)
        s_mm = Sem(nc, st, "C_mm")
        s_evA = Sem(nc, st, "C_evA")
        s_evD = Sem(nc, st, "C_evD")
        s_sq = Sem(nc, st, "C_sq")
        s_out = Sem(nc, st, "C_out")

        P.dma("sp", ident[:], ident_d, s_c)
        for t_, d_ in ((g1t, g1), (b1t, b1), (g2t, g2), (b2t, b2)):
            P.dma("sp", t_[:], d_, s_c)
        for e_ in ("pe", "act", "dve"):
            P.wait(e_, s_c, s_c.n)

        wo_view = w_out.rearrange("(c p) e -> p c e", p=128)
        wu_view = w_up.rearrange("(c p) e -> p c e", p=128)
        wd_view = w_down.rearrange("(c p) e -> p c e", p=128)

        mm = MM(P, mps, s_mm)
        blk_end = []
        nxb = 0
        nrt = 0

        def load_w(view, r0, nr, c0, ncol):
            nblk = len(blk_end)
            wbuf = wbl[nblk % 2]
            if nblk >= 2:
                P.wait("pool", s_mm, blk_end[nblk - 2])
            wv = wbuf[:, 0:nr * ncol].rearrange("p (c e) -> p c e", c=nr)
            vwl = P.dma("pool", wv, view[:, r0:r0 + nr, c0:c0 + ncol], s_wl[nblk])
            P.wait("pe", s_wl[nblk], vwl)
            return wv

        for sti in range(NST):
            t0 = sti * ST
            if sti > 0:
                P.wait("sp", s_out, s_out.n)
                P.wait("sp", s_mm, s_mm.n)
            P.dma("sp", mTs[:], mT[:, :, t0:t0 + ST].rearrange("c p t -> p c t"), s_in)
            for tt in range(4):
                P.dma("sp", acc[tt][:], hres[t0 + tt * 128:t0 + (tt + 1) * 128, :], s_in)
            in_ready = s_in.n
            P.wait("pe", s_in, in_ready)
            P.wait("dve", s_in, in_ready)

            for wb in range(4):
                wv = load_w(wo_view, 0, 16, wb * 512, 512)
                for tt in range(4):
                    bank, vm, idx = mm.group(16, lambda bank, c, wv=wv, tt=tt: (lambda e: e.matmul(
                        bank[:], mTs[:, c, tt * 128:(tt + 1) * 128], wv[:, c, :], start=(c == 0), stop=(c == 15))))
                    P.wait("dve", s_mm, vm)
                    a = acc[tt]
                    vv = P.op("dve", lambda e, a=a, bank=bank, wb=wb: e.scalar_tensor_tensor(
                        a[:, wb * 512:(wb + 1) * 512], a[:, wb * 512:(wb + 1) * 512], ALPHA, bank[:], ALU.mult, ALU.add), s_evD)
                    mm.done(idx, s_evD, vv)
                blk_end.append(s_mm.n)
            for tt in range(4):
                a = acc[tt]
                P.wait("dve", s_evD, s_evD.n)
                vh = emit_ln(P, a, g1t, b1t, scr, s_lnA, s_lnD, s_h)
                xb = xbs[nxb % 2]
                P.wait("act", s_h, vh)
                if nxb >= 2:
                    P.wait("act", s_tp, nxb - 1)
                vb = P.op("act", lambda e, xb=xb, a=a: e.activation(out=xb[:], in_=a[:], func=AF.Copy), s_xb)
                P.wait("pe", s_xb, vb)
                P.wait("pe", s_te, 2 * nxb)
                for c in range(16):
                    tp = tps[c // 8]
                    fn = lambda e, tp=tp, c=c, xb=xb: e.transpose(tp[:, (c % 8) * 128:(c % 8 + 1) * 128], xb[:, c * 128:(c + 1) * 128], ident[:])
                    if c == 15:
                        vt = P.op("pe", fn, s_tp)
                    else:
                        P.op("pe", fn)
                P.wait("act", s_tp, vt)
                P.wait("dve", s_tp, vt)
                P.op("act", lambda e, tt=tt: e.activation(out=h1T[:, 0:8, tt * 128:(tt + 1) * 128],
                                                         in_=tps[0][:].rearrange("p (c t) -> p c t", c=8), func=AF.Copy), s_te)
                P.op("dve", lambda e, tt=tt: e.tensor_copy(h1T[:, 8:16, tt * 128:(tt + 1) * 128],
                                                          tps[1][:].rearrange("p (c t) -> p c t", c=8)), s_te)
                nxb += 1
            h1T_ready = s_te.n
            P.wait("pe", s_te, h1T_ready)
            for fh in range(2):
                aT_free = s_mm.n
                for ub in range(4):
                    wv = load_w(wu_view, 0, 16, fh * 4096 + ub * 1024, 1024)
                    for fc in range(8):
                        bank, vm, idx = mm.group(16, lambda bank, c, wv=wv, fc=fc: (lambda e: e.matmul(
                            bank[:], wv[:, c, fc * 128:(fc + 1) * 128], h1T[:, c, :], start=(c == 0), stop=(c == 15))))
                        rt = rtmp[nrt % 2]
                        P.wait("act", s_mm, vm)
                        if nrt >= 2:
                            P.wait("act", s_sq, nrt - 1)
                        vr = P.op("act", lambda e, rt=rt, bank=bank: e.activation(out=rt[:], in_=bank[:], func=AF.Relu), s_evA)
                        mm.done(idx, s_evA, vr)
                        P.wait("dve", s_evA, vr)
                        if ub == 0 and fc == 0:
                            P.wait("dve", s_mm, aT_free)
                        P.op("dve", lambda e, rt=rt, ub=ub, fc=fc: e.tensor_tensor(aT[:, ub * 8 + fc, :], rt[:], rt[:], ALU.mult), s_sq)
                        nrt += 1
                    blk_end.append(s_mm.n)
                aT_ready = s_sq.n
                for db in range(4):
                    wv = load_w(wd_view, fh * 32, 32, db * 512, 512)
                    if db == 0:
                        P.wait("pe", s_sq, aT_ready)
                    for tt in range(4):
                        bank, vm, idx = mm.group(32, lambda bank, c, wv=wv, tt=tt: (lambda e: e.matmul(
                            bank[:], aT[:, c, tt * 128:(tt + 1) * 128], wv[:, c, :], start=(c == 0), stop=(c == 31))))
                        P.wait("dve", s_mm, vm)
                        a = acc[tt]
                        if fh == 0:
                            if db == 0 and tt == 0:
                                P.wait("dve", s_xb, s_xb.n)
                            vv = P.op("dve", lambda e, a=a, bank=bank, db=db: e.scalar_tensor_tensor(
                                a[:, db * 512:(db + 1) * 512], a[:, db * 512:(db + 1) * 512], ALPHA, bank[:], ALU.mult, ALU.add), s_evD)
                        else:
                            vv = P.op("dve", lambda e, a=a, bank=bank, db=db: e.tensor_tensor(
                                a[:, db * 512:(db + 1) * 512], a[:, db * 512:(db + 1) * 512], bank[:], ALU.add), s_evD)
                        mm.done(idx, s_evD, vv)
                    blk_end.append(s_mm.n)
            for tt in range(4):
                a = acc[tt]
                P.wait("dve", s_evD, s_evD.n)
                vh = emit_ln(P, a, g2t, b2t, scr, s_lnA, s_lnD, s_h)
                P.wait("sp", s_h, vh)
                P.dma("sp", hout[t0 + tt * 128:t0 + (tt + 1) * 128, :], a[:], s_out)
        P.wait("sp", s_out, s_out.n)
        P.run()


def phase_B(nc, qT, kT, v, cst, mT, n_sb, n_df, lam_init):
    NG = S // 512
    with ExitStack() as st:
        P = Prog(nc)
        ident = _sb(nc, st, "B_id", [128, 128], BF16)
        uinc = _sb(nc, st, "B_uinc", [128, 128], BF16)
        lstr = _sb(nc, st, "B_lstr", [128, 128], BF16)
        onesb = _sb(nc, st, "B_onesb", [128, 128], BF16)
        ones32 = _sb(nc, st, "B_ones32", [128, 128], F32)
        msb = _sb(nc, st, "B_msb", [128, 128], BF16)
        mdf = _sb(nc, st, "B_mdf", [128, 128], F32)
        sbg = _sb(nc, st, "B_sbg", [128, 1], F32)
        dfg = _sb(nc, st, "B_dfg", [128, 2], F32)
        lamv = _sb(nc, st, "B_lamv", [128, 4, 128], F32)
        lamt = _sb(nc, st, "B_lamt", [128, 2, 128], F32)
        lams = _sb(nc, st, "B_lams", [128, 4], F32)
        neglam = _sb(nc, st, "B_neglam", [128, 1], F32)
        biasD = _sb(nc, st, "B_biasD", [128, max(n_df, 1), 128], F32)
        biasP = _sb(nc, st, "B_biasP", [128, max(n_df, 1), 128], F32)
        relb = _sb(nc, st, "B_relb", [128, max(n_df, 1), 32], F32)
        b31 = _sb(nc, st, "B_b31", [128, max(n_df, 1)], F32)
        bmx = _sb(nc, st, "B_bmx", [128, max(n_df, 1)], F32)
        bD_hi = _sb(nc, st, "B_bDhi", [128, max(n_df, 1), 128], BF16)
        bD_lo = _sb(nc, st, "B_bDlo", [128, max(n_df, 1), 128], BF16)
        bP_hi = _sb(nc, st, "B_bPhi", [128, max(n_df, 1), 128], BF16)
        bP_lo = _sb(nc, st, "B_bPlo", [128, max(n_df, 1), 128], BF16)
        btmp = _sb(nc, st, "B_btmp", [128, 128], F32)

        s_c = Sem(nc, st, "B_c")
        s_su = Sem(nc, st, "B_su")
        s_ld = SemRing(nc, st, "B_ld", 2)
        s_out = SemRing(nc, st, "B_out", 2)

        for t_, k_ in ((ident, "ident"), (uinc, "uinc"), (lstr, "lstr"), (onesb, "onesb"), (ones32, "ones32"),
                       (msb, "msb"), (mdf, "mdf"), (sbg, "sbg"), (dfg, "dfg"), (lamv, "lamv")):
            P.dma("sp", t_[:], cst[k_], s_c)
        if n_df:
            for t_, k_ in ((biasD, "biasD"), (biasP, "biasP"), (relb, "relb"), (b31, "b31")):
                P.dma("sp", t_[:], cst[k_], s_c)
        for e_ in ("pe", "act", "dve"):
            P.wait(e_, s_c, s_c.n)

        def dve_chain(fn):
            v_ = P.op("dve", fn, s_su)
            P.wait("dve", s_su, v_)
            return v_

        if n_df:
            dve_chain(lambda e: e.tensor_tensor(lamt[:, 0, :], lamv[:, 0, :], lamv[:, 1, :], ALU.mult))
            dve_chain(lambda e: e.tensor_tensor(lamt[:, 1, :], lamv[:, 2, :], lamv[:, 3, :], ALU.mult))
            dve_chain(lambda e: e.reduce_sum(lams[:, 0:2], lamt[:], axis=mybir.AxisListType.X))
            P.wait("act", s_su, s_su.n)
            va = P.op("act", lambda e: e.activation(out=lams[:, 2:4], in_=lams[:, 0:2], func=AF.Exp), s_su)
            P.wait("dve", s_su, va)
            dve_chain(lambda e: e.tensor_tensor(neglam[:], lams[:, 3:4], lams[:, 2:3], ALU.subtract))
            dve_chain(lambda e: e.tensor_scalar(neglam[:], neglam[:], -float(lam_init), None, ALU.add))
            dve_chain(lambda e: e.reduce_max(bmx[:], relb[:], axis=mybir.AxisListType.X))
            dve_chain(lambda e: e.tensor_tensor(bmx[:], bmx[:], b31[:], ALU.subtract))
            for h in range(n_df):
                dve_chain(lambda e, h=h: e.scalar_tensor_tensor(btmp[:], biasD[:, h, :], b31[:, h:h + 1], mdf[:], ALU.subtract, ALU.add))
                dve_chain(lambda e, h=h: e.tensor_copy(bD_hi[:, h, :], btmp[:]))
                dve_chain(lambda e, h=h: e.tensor_tensor(bD_lo[:, h, :], btmp[:], bD_hi[:, h, :], ALU.subtract))
                dve_chain(lambda e, h=h: e.tensor_scalar(btmp[:], biasP[:, h, :], b31[:, h:h + 1], None, ALU.subtract))
                dve_chain(lambda e, h=h: e.tensor_copy(bP_hi[:, h, :], btmp[:]))
                dve_chain(lambda e, h=h: e.tensor_tensor(bP_lo[:, h, :], btmp[:], bP_hi[:, h, :], ALU.subtract))
            P.wait("pe", s_su, s_su.n)
            P.wait("act", s_su, s_su.n)

        if n_sb:
            with ExitStack() as s2:
                kq = [[_sb(nc, s2, f"S_k{i}", [128, S], BF16), _sb(nc, s2, f"S_q{i}", [128, S], BF16),
                       _sb(nc, s2, f"S_v{i}", [128, S // 128, 128], BF16)] for i in range(4)]
                eS = [_sb(nc, s2, f"S_e{i}", [128, 512], F32) for i in range(4)]
                spS = [_sb(nc, s2, f"S_sp{i}", [128, 512], BF16) for i in range(4)]
                xS = [_sb(nc, s2, f"S_x{i}", [128, 512], F32) for i in range(4)]
                wS = [_sb(nc, s2, f"S_w{i}", [128, 512], BF16) for i in range(4)]
                fsq = [_sb(nc, s2, f"S_fsq{i}", [128, 512], BF16) for i in range(2)]
                fo = [_sb(nc, s2, f"S_fo{i}", [128, 512], F32) for i in range(2)]
                frs = [_sb(nc, s2, f"S_frs{i}", [128, 512], F32) for i in range(2)]
                mo = [_sb(nc, s2, f"S_mo{i}", [128, 512], BF16) for i in range(2)]
                zps = [_ps(nc, s2, f"S_z{i}", [128, 512]) for i in range(2)]
                aps = [_ps(nc, s2, f"S_a{i}", [128, 512]) for i in range(2)]
                ops_ = [_ps(nc, s2, f"S_o{i}", [128, 512]) for i in range(2)]
                ssq = _ps(nc, s2, "S_ssq", [128, 512])

                s_z = Sem(nc, s2, "S_sz")
                s_e = Sem(nc, s2, "S_se")
                s_sp = Sem(nc, s2, "S_ssp")
                s_U = Sem(nc, s2, "S_sU")
                s_x = Sem(nc, s2, "S_sx")
                s_w = Sem(nc, s2, "S_sw")
                s_pv = Sem(nc, s2, "S_spv")
                s_f1a = Sem(nc, s2, "S_f1a")
                s_f1d = Sem(nc, s2, "S_f1d")
                s_f2 = Sem(nc, s2, "S_f2")
                s_f3 = Sem(nc, s2, "S_f3")
                s_hd = Sem(nc, s2, "S_hd")
                s_f4 = Sem(nc, s2, "S_f4")
                s_f5 = Sem(nc, s2, "S_f5")

                npairs = n_sb // 2
                load_vals = {}

                def load_pair(pi):
                    if pi >= 2:
                        P.wait("sp", s_f2, (pi - 1) * 2 * NG)
                    for j in range(2):
                        hh = 2 * pi + j
                        kt, qt, vt = kq[(pi % 2) * 2 + j]
                        P.dma("sp", kt[:], kT[hh, :, :], s_ld[pi])
                        P.dma("sp", qt[:], qT[hh, :, :], s_ld[pi])
                        P.dma("sp", vt[:], v[:, hh * 128:(hh + 1) * 128].rearrange("(b p) d -> p b d", p=128), s_ld[pi])
                    load_vals[pi] = s_ld[pi].n

                its = []
                for pi in range(npairs):
                    for g in range(NG):
                        for kb in range(4 * g + 3, -1, -1):
                            for j in range(2):
                                its.append(dict(pi=pi, slot=j, h=2 * pi + j, g=g, kb=kb, first=(kb == 4 * g + 3), last=(kb == 0)))
                N = len(its)
                load_pair(0)
                if npairs > 1:
                    load_pair(1)
                deferred = {}
                nfin = [0]
                slot_fin = {0: 0, 1: 0}
                fin_evac_val = {0: [0, 0], 1: [0, 0]}

                def cols_of(it):
                    m = it["kb"] - 4 * it["g"]
                    c0 = max(m, 0) * 128
                    return m, c0

                def stage0(i):
                    it = its[i]
                    kt, qt, vt = kq[(it["pi"] % 2) * 2 + it["slot"]]
                    m, c0 = cols_of(it)
                    g, kb, sl, ws = it["g"], it["kb"], it["slot"], i % 4
                    if it["first"] and it["g"] == 0:
                        P.wait("pe", s_ld[it["pi"]], load_vals[it["pi"]])
                    P.wait("pe", s_e, i - 1)
                    z = zps[sl]
                    if m >= 0:
                        P.op("pe", lambda e: e.matmul(z[:, c0:512], kt[:, kb * 128:(kb + 1) * 128], qt[:, g * 512 + c0:(g + 1) * 512],
                                                      start=True, stop=False, skip_group_check=True))
                        vz = P.op("pe", lambda e: e.matmul(z[:, c0:c0 + 128], ident[:], msb[:], start=False, stop=True, skip_group_check=True), s_z)
                    else:
                        vz = P.op("pe", lambda e: e.matmul(z[:, 0:512], kt[:, kb * 128:(kb + 1) * 128], qt[:, g * 512:(g + 1) * 512],
                                                           start=True, stop=True, skip_group_check=True), s_z)
                    P.wait("act", s_z, vz)
                    ve = P.op("act", lambda e: e.activation(out=eS[ws][:, c0:512], in_=z[:, c0:512], func=AF.Exp), s_e)
                    P.wait("act", s_e, ve)
                    P.op("act", lambda e: e.activation(out=spS[ws][:, c0:512], in_=eS[ws][:, c0:512], func=AF.Ln, bias=1.0, scale=1.0), s_sp)

                def stage1(i):
                    it = its[i]
                    m, c0 = cols_of(it)
                    sl, ws = it["slot"], i % 4
                    a = aps[sl]
                    P.wait("pe", s_sp, i + 1)
                    P.wait("pe", s_x, i - 1)
                    vu = P.op("pe", lambda e: e.matmul(a[:, c0:512], uinc[:], spS[ws][:, c0:512], start=it["first"], stop=True, skip_group_check=True), s_U)
                    P.wait("act", s_U, vu)
                    vx = P.op("act", lambda e: e.activation(out=xS[ws][:, c0:512], in_=a[:, c0:512], func=AF.Exp, scale=-1.0), s_x)
                    P.wait("dve", s_x, vx)
                    P.op("dve", lambda e: e.tensor_tensor(wS[ws][:, c0:512], xS[ws][:, c0:512], eS[ws][:, c0:512], ALU.mult), s_w)

                def stage2(i):
                    it = its[i]
                    kt, qt, vt = kq[(it["pi"] % 2) * 2 + it["slot"]]
                    m, c0 = cols_of(it)
                    sl, ws, kb = it["slot"], i % 4, it["kb"]
                    a = aps[sl]
                    o = ops_[sl]
                    if not it["last"]:
                        P.wait("pe", s_x, i + 1)
                        P.op("pe", lambda e: e.matmul(a[:, c0:512], lstr[:], spS[ws][:, c0:512], start=False, stop=True, skip_group_check=True))
                    P.wait("pe", s_w, i + 1)
                    if it["first"]:
                        P.wait("pe", s_f1a, fin_evac_val[sl][0])
                        P.wait("pe", s_f1d, fin_evac_val[sl][1])
                    vp = P.op("pe", lambda e: e.matmul(o[:, c0:512], vt[:, kb, :], wS[ws][:, c0:512], start=it["first"], stop=True, skip_group_check=True), s_pv)
                    if it["last"]:
                        k = nfin[0]
                        nfin[0] += 1
                        g, h = it["g"], it["h"]
                        pair_done = (it["g"] == NG - 1 and sl == 1)
                        pi = it["pi"]

                        def fin_a():
                            P.wait("dve", s_pv, vp)
                            vd_ = P.op("dve", lambda e: e.tensor_copy(fo[sl][:], o[:]), s_f1d)
                            P.wait("act", s_f1d, vd_)
                            va_ = P.op("act", lambda e: e.activation(out=fsq[sl][:], in_=fo[sl][:], func=AF.Square), s_f1a)
                            fin_evac_val[sl][0] = va_
                            fin_evac_val[sl][1] = vd_
                            return va_, vd_

                        def fin_b(vals):
                            P.wait("pe", s_f1a, vals[0])
                            P.wait("pe", s_f3, k)
                            vq = P.op("pe", lambda e: e.matmul(ssq[:], onesb[:], fsq[sl][:], start=True, stop=True), s_f2)
                            return vq

                        ctx = {}

                        def d1():
                            ctx["vals"] = fin_a()

                        def d2():
                            ctx["vq"] = fin_b(ctx["vals"])

                        def d3():
                            vq = ctx["vq"]
                            P.wait("dve", s_f2, vq)
                            P.wait("dve", s_f1d, ctx["vals"][1])
                            P.wait("act", s_f2, vq)
                            vv = P.op("act", lambda e: e.activation(out=frs[sl][:], in_=ssq[:], func=AF.Ln, bias=RMS_EPS, scale=1.0 / HD), s_f3)
                            P.wait("act", s_f3, vv)
                            vv = P.op("act", lambda e: e.activation(out=frs[sl][:], in_=frs[sl][:], func=AF.Exp, scale=-0.5), s_f4)
                            P.wait("dve", s_f4, vv)
                            P.wait("dve", s_out[k], s_out[k].n)
                            vv2 = P.op("dve", lambda e: e.scalar_tensor_tensor(mo[k % 2][:], fo[sl][:], sbg[:, 0:1], frs[sl][:], ALU.mult, ALU.mult), s_f5)
                            P.wait("sp", s_f5, vv2)
                            P.dma("sp", mT[h, :, g * 512:(g + 1) * 512], mo[k % 2][:], s_out[k])

                        deferred.setdefault(i + 2 + 1, []).append(d1)
                        deferred.setdefault(i + 2 + 2, []).append(d2)
                        deferred.setdefault(i + 2 + 3, []).append(d3)

                for step in range(N + 8):
                    if step < N:
                        it = its[step]
                        if it["first"] and it["g"] == 0 and it["slot"] == 0 and it["pi"] >= 1 and it["pi"] + 1 < npairs:
                            load_pair(it["pi"] + 1)
                        stage0(step)
                    if 0 <= step - 1 < N:
                        stage1(step - 1)
                    if 0 <= step - 2 < N:
                        stage2(step - 2)
                    for fn in deferred.pop(step, []):
                        fn()
                assert not deferred
                P.wait_all("sp", s_out)
                P.wait("pe", s_f5, s_f5.n)
                P.wait("act", s_f5, s_f5.n)
                P.wait_all("dve", s_out)
                P.run()
                P = Prog(nc)

        if n_df:
            with ExitStack() as s2:
                kqd = [[_sb(nc, s2, f"D_k{i}_{m}", [128, S], BF16) for m in range(2)] +
                       [_sb(nc, s2, f"D_q{i}_{m}", [128, S], BF16) for m in range(2)] +
                       [_sb(nc, s2, f"D_v{i}", [128, S // 128, 256], BF16)] for i in range(2)]
                pS = [_sb(nc, s2, f"D_p{i}", [128, 512], BF16) for i in range(4)]
                sqt = [_sb(nc, s2, f"D_sqt{i}", [128, 512], BF16) for i in range(2)]
                mxs = _sb(nc, s2, "D_mxs", [128, 2 * NG], F32)
                mred = _sb(nc, s2, "D_mred", [128, 4], F32)
                cb = [_sb(nc, s2, f"D_cb{i}", [128, 2], F32) for i in range(2)]
                rden = _sb(nc, s2, "D_rden", [128, 512], F32)
                nrm = [[_sb(nc, s2, f"D_nrm{m}_{hf}", [128, 512], F32) for hf in range(2)] for m in range(2)]
                osb = [_sb(nc, s2, f"D_o{hf}", [128, 512], F32) for hf in range(2)]
                osq = [_sb(nc, s2, f"D_osq{hf}", [128, 512], F32) for hf in range(2)]
                drs = _sb(nc, s2, "D_rs", [128, 512], F32)
                dmo = [_sb(nc, s2, f"D_mo{i}", [128, 512], BF16) for i in range(4)]
                dgs = _sb(nc, s2, "D_gs", [128, 2], F32)
                lps = [_ps(nc, s2, f"D_l{i}", [128, 512]) for i in range(2)]
                den = _ps(nc, s2, "D_den", [128, 512])
                num = [_ps(nc, s2, f"D_num{i}", [128, 512]) for i in range(2)]
                aux = _ps(nc, s2, "D_aux", [128, 512])

                s_l = Sem(nc, s2, "D_sl")
                s_p = Sem(nc, s2, "D_sp")
                s_acc = Sem(nc, s2, "D_sacc")
                s_fin = Sem(nc, s2, "D_sfin")
                s_d = Sem(nc, s2, "D_sd")
                s_a = Sem(nc, s2, "D_sa")
                s_q = Sem(nc, s2, "D_sq")
                s_hd = Sem(nc, s2, "D_hd")
                s_do = SemRing(nc, s2, "D_do", 4)
                s_hq = Sem(nc, s2, "D_hq")

                v_gs = P.op("dve", lambda e: e.tensor_scalar(dgs[:], dfg[:], 1.0 - float(lam_init), None, ALU.mult), s_d)

                load_vals = {}

                def load_head(h):
                    if h >= 2:
                        P.wait("sp", s_fin, (h - 1) * 2 * NG)
                    bufs = kqd[h % 2]
                    for m in range(2):
                        u = n_sb + 2 * h + m
                        P.dma("sp", bufs[m][:], kT[u, :, :], s_ld[h])
                        P.dma("sp", bufs[2 + m][:], qT[u, :, :], s_ld[h])
                    P.dma("sp", bufs[4][:], v[:, n_sb * 128 + h * 256: n_sb * 128 + (h + 1) * 256].rearrange("(b p) d -> p b d", p=128), s_ld[h])
                    load_vals[h] = s_ld[h].n

                naux = [0]

                def head_setup(h):
                    bufs = kqd[h % 2]
                    P.wait("act", s_ld[h], load_vals[h])
                    P.wait("pe", s_ld[h], load_vals[h])
                    for m in range(2):
                        for which in range(2):
                            src = bufs[m] if which == 0 else bufs[2 + m]
                            for ch in range(NG):
                                n_ = naux[0]
                                sq_ = sqt[n_ % 2]
                                if n_ >= 2:
                                    P.wait("act", s_hq, n_ - 1)
                                va_ = P.op("act", lambda e, sq_=sq_, src=src, ch=ch: e.activation(out=sq_[:], in_=src[:, ch * 512:(ch + 1) * 512], func=AF.Square), s_a)
                                P.wait("pe", s_a, va_)
                                P.wait("pe", s_d, s_d.n)
                                vq_ = P.op("pe", lambda e, sq_=sq_: e.matmul(aux[:], onesb[:], sq_[:], start=True, stop=True), s_hq)
                                P.wait("dve", s_hq, vq_)
                                vd_ = P.op("dve", lambda e, which=which, ch=ch: e.reduce_max(mxs[:, which * NG + ch: which * NG + ch + 1], aux[:], axis=mybir.AxisListType.X), s_d)
                                P.wait("dve", s_d, vd_)
                                naux[0] += 1
                        def dch(fn):
                            vv = P.op("dve", fn, s_d)
                            P.wait("dve", s_d, vv)
                            return vv
                        dch(lambda e: e.reduce_max(mred[:, 0:2], mxs[:].rearrange("p (w c) -> p w c", w=2), axis=mybir.AxisListType.X))
                        dch(lambda e: e.tensor_tensor(mred[:, 2:3], mred[:, 0:1], mred[:, 1:2], ALU.mult))
                        P.wait("act", s_d, s_d.n)
                        va_ = P.op("act", lambda e: e.activation(out=mred[:, 3:4], in_=mred[:, 2:3], func=AF.Ln), s_a)
                        P.wait("act", s_a, va_)
                        va_ = P.op("act", lambda e: e.activation(out=mred[:, 3:4], in_=mred[:, 3:4], func=AF.Exp, scale=0.5), s_a)
                        P.wait("dve", s_a, va_)
                        dch(lambda e: e.tensor_scalar(mred[:, 3:4], mred[:, 3:4], -1.02, None, ALU.mult))
                        vv = dch(lambda e, m=m, h=h: e.tensor_tensor(cb[h % 2][:, m:m + 1], mred[:, 3:4], bmx[:, h:h + 1], ALU.subtract))
                    P.wait("act", s_d, s_d.n)
                    P.wait("pe", s_d, s_d.n)

                its = []
                chains = []
                for h in range(n_df):
                    for g in range(NG):
                        for m in range(2):
                            ci = len(chains)
                            chains.append(dict(h=h, g=g, m=m))
                            for kb in range(0, 4 * g + 4):
                                its.append(dict(ci=ci, h=h, g=g, m=m, kb=kb, first=(kb == 0), last=(kb == 4 * g + 3)))
                N = len(its)
                load_head(0)
                if n_df > 1:
                    load_head(1)
                deferred = {}
                nfin2 = [0]
                ndo = [0]

                def dstage0(i):
                    it = its[i]
                    h, g, m, kb = it["h"], it["g"], it["m"], it["kb"]
                    bufs = kqd[h % 2]
                    kt, qt = bufs[m], bufs[2 + m]
                    mm_ = kb - 4 * g
                    c0 = max(mm_, 0) * 128
                    lb = lps[i % 2]
                    diag = mm_ >= 0
                    prev = (mm_ + 1 >= 0) and (mm_ + 1 <= 3)
                    P.wait("pe", s_p, i - 1)
                    n_extra = (2 if diag else 0) + (2 if prev else 0)
                    fn0 = lambda e: e.matmul(lb[:, c0:512], kt[:, kb * 128:(kb + 1) * 128], qt[:, g * 512 + c0:(g + 1) * 512],
                                             start=True, stop=(n_extra == 0), skip_group_check=True)
                    if n_extra == 0:
                        vl = P.op("pe", fn0, s_l)
                    else:
                        P.op("pe", fn0)
                        extra = []
                        if diag:
                            extra += [(c0, bD_hi), (c0, bD_lo)]
                        if prev:
                            cp = (mm_ + 1) * 128
                            extra += [(cp, bP_hi), (cp, bP_lo)]
                        for j, (cc, bt_) in enumerate(extra):
                            fn = lambda e, cc=cc, bt_=bt_, j=j: e.matmul(lb[:, cc:cc + 128], ident[:], bt_[:, h, :], start=False,
                                                                        stop=(j == len(extra) - 1), skip_group_check=True)
                            if j == len(extra) - 1:
                                vl = P.op("pe", fn, s_l)
                            else:
                                P.op("pe", fn)
                    P.wait("act", s_l, vl)
                    P.op("act", lambda e: e.activation(out=pS[i % 4][:, c0:512], in_=lb[:, c0:512], func=AF.Exp,
                                                       bias=cb[h % 2][:, m:m + 1], scale=1.0), s_p)

                def dstage1(i):
                    it = its[i]
                    h, g, m, kb = it["h"], it["g"], it["m"], it["kb"]
                    bufs = kqd[h % 2]
                    vt = bufs[4]
                    mm_ = kb - 4 * g
                    c0 = max(mm_, 0) * 128
                    p_ = pS[i % 4]
                    P.wait("pe", s_p, i + 1)
                    if it["first"]:
                        P.wait("pe", s_fin, it["ci"])
                    st_ = it["first"]
                    P.op("pe", lambda e: e.matmul(den[:, c0:512], onesb[:], p_[:, c0:512], start=st_, stop=True, skip_group_check=True))
                    P.op("pe", lambda e: e.matmul(num[0][:, c0:512], vt[:, kb, 0:128], p_[:, c0:512], start=st_, stop=True, skip_group_check=True))
                    vacc = P.op("pe", lambda e: e.matmul(num[1][:, c0:512], vt[:, kb, 128:256], p_[:, c0:512], start=st_, stop=True, skip_group_check=True), s_acc)
                    if it["last"]:
                        P.wait("dve", s_acc, vacc)
                        def dch(fn, sem=s_d):
                            vv = P.op("dve", fn, sem)
                            P.wait("dve", sem, vv)
                            return vv
                        dch(lambda e: e.reciprocal(rden[:], den[:]))
                        dch(lambda e: e.tensor_tensor(nrm[m][0][:], num[0][:], rden[:], ALU.mult))
                        dch(lambda e: e.tensor_tensor(nrm[m][1][:], num[1][:], rden[:], ALU.mult), s_fin)
                        if m == 1:
                            k = nfin2[0]
                            nfin2[0] += 1
                            head_done = (g == NG - 1)
                            for hf in range(2):
                                dch(lambda e, hf=hf: e.scalar_tensor_tensor(osb[hf][:], nrm[1][hf][:], neglam[:, 0:1], nrm[0][hf][:], ALU.mult, ALU.add))
                            vo = s_d.n
                            ctx = {}

                            def d1():
                                P.wait("act", s_d, vo)
                                P.op("act", lambda e: e.activation(out=osq[0][:], in_=osb[0][:], func=AF.Square), s_a)
                                ctx["va"] = P.op("act", lambda e: e.activation(out=osq[1][:], in_=osb[1][:], func=AF.Square), s_a)

                            def d2():
                                P.wait("pe", s_a, ctx["va"])
                                P.wait("pe", s_d, s_d.n)
                                P.op("pe", lambda e: e.matmul(aux[:], ones32[:], osq[0][:], start=True, stop=False))
                                ctx["vq"] = P.op("pe", lambda e: e.matmul(aux[:], ones32[:], osq[1][:], start=False, stop=True), s_q)

                            def d3():
                                P.wait("dve", s_q, ctx["vq"])
                                P.wait("act", s_q, ctx["vq"])
                                va_ = P.op("act", lambda e: e.activation(out=drs[:], in_=aux[:], func=AF.Ln, bias=RMS_EPS, scale=1.0 / (2 * HD)), s_a)
                                P.wait("act", s_a, va_)
                                va_ = P.op("act", lambda e: e.activation(out=drs[:], in_=drs[:], func=AF.Exp, scale=-0.5), s_a)
                                P.wait("dve", s_a, va_)
                                for hf in range(2):
                                    n_ = ndo[0]
                                    ndo[0] += 1
                                    P.wait("dve", s_do[n_], s_do[n_].n)
                                    vv = dch(lambda e, hf=hf, n_=n_: e.scalar_tensor_tensor(dmo[n_ % 4][:], osb[hf][:], dgs[:, hf:hf + 1], drs[:], ALU.mult, ALU.mult))
                                    P.wait("sp", s_d, vv)
                                    P.dma("sp", mT[n_sb + 2 * h + hf, :, g * 512:(g + 1) * 512], dmo[n_ % 4][:], s_do[n_])

                            deferred.setdefault(i + 1 + 2, []).append(d1)
                            deferred.setdefault(i + 1 + 4, []).append(d2)
                            deferred.setdefault(i + 1 + 6, []).append(d3)

                cur_h = -1
                for step in range(N + 10):
                    if step < N:
                        it = its[step]
                        if it["h"] != cur_h:
                            cur_h = it["h"]
                            if cur_h >= 1 and cur_h + 1 < n_df:
                                load_head(cur_h + 1)
                            head_setup(cur_h)
                        dstage0(step)
                    if 0 <= step - 1 < N:
                        dstage1(step - 1)
                    for fn in deferred.pop(step, []):
                        fn()
                assert not deferred
                P.wait_all("sp", s_do)
                P.wait("pe", s_d, s_d.n)
                P.wait("act", s_d, s_d.n)
                P.wait_all("dve", s_do)
                P.run()
                P = Prog(nc)


def _t5_bucket(n):
    n = np.maximum(n, 0)
    max_exact = 16
    nf = np.maximum(n, 1).astype(np.float32)
    large = max_exact + (np.log(nf / max_exact) / math.log(128 / max_exact) * (32 - max_exact)).astype(np.int32)
    large = np.minimum(large, 31)
    return np.where(n < max_exact, n, large)


def _consts():
    bf = ml_dtypes.bfloat16
    j = np.arange(128)[:, None]
    s_ = np.arange(128)[None, :]
    c = {}
    c["ident"] = np.eye(128, dtype=np.float32).astype(bf)
    c["uinc"] = (j >= s_).astype(np.float32).astype(bf)
    c["lstr"] = (j < s_).astype(np.float32).astype(bf)
    c["onesb"] = np.ones((128, 128), np.float32).astype(bf)
    c["ones32"] = np.ones((128, 128), np.float32)
    c["msb"] = np.where(j >= s_, NEG, 0.0).astype(np.float32).astype(bf)
    c["mdf"] = np.where(j > s_, NEG, 0.0).astype(np.float32)
    return c


_BUCK_D = _t5_bucket(np.arange(128)[None, :] - np.arange(128)[:, None])
_BUCK_P = _t5_bucket(128 + np.arange(128)[None, :] - np.arange(128)[:, None])


def _layer_consts(inputs, l, heads_df):
    c = {}
    c["sbg"] = np.ascontiguousarray(inputs["sb_norm_g"][l].reshape(128, 1)).astype(np.float32)
    c["dfg"] = np.ascontiguousarray(inputs["diff_norm_g"][l].reshape(2, 128).T).astype(np.float32)
    lamv = np.stack([inputs["lam_q1"][l], inputs["lam_k1"][l], inputs["lam_q2"][l], inputs["lam_k2"][l]], 0)
    c["lamv"] = np.ascontiguousarray(np.broadcast_to(lamv[None], (128, 4, 128))).astype(np.float32)
    rb = inputs["rel_bias"]
    nd = len(heads_df)
    c["biasD"] = np.ascontiguousarray(np.stack([rb[_BUCK_D, h] for h in heads_df], 1)).astype(np.float32)
    c["biasP"] = np.ascontiguousarray(np.stack([rb[_BUCK_P, h] for h in heads_df], 1)).astype(np.float32)
    c["relb"] = np.ascontiguousarray(np.broadcast_to(np.stack([rb[:, h] for h in heads_df], 0)[None], (128, nd, 32))).astype(np.float32)
    c["b31"] = np.ascontiguousarray(np.broadcast_to(np.stack([rb[31, h] for h in heads_df], 0)[None], (128, nd))).astype(np.float32)
    return c


def _bcast(vec):
    return np.ascontiguousarray(np.broadcast_to(vec[None, :], (128, vec.shape[0]))).astype(np.float32)


_B_CONST_SPECS = [("ident", [128, 128], BF16), ("uinc", [128, 128], BF16), ("lstr", [128, 128], BF16),
                  ("onesb", [128, 128], BF16), ("ones32", [128, 128], F32), ("msb", [128, 128], BF16),
                  ("mdf", [128, 128], F32), ("sbg", [128, 1], F32), ("dfg", [128, 2], F32), ("lamv", [128, 4, 128], F32)]


def _b_specs(nd):
    return _B_CONST_SPECS + [("biasD", [128, nd, 128], F32), ("biasP", [128, nd, 128], F32),
                             ("relb", [128, nd, 32], F32), ("b31", [128, nd], F32)]


def build_A(NT, do_ln):
    nc = bass.Bass("TRN2", target_bir_lowering=False)
    xin = nc.dram_tensor("xin", [NT, D], F32, kind="ExternalInput").ap()
    w_in = nc.dram_tensor("w_in", [D, INW], F32, kind="ExternalInput").ap()
    lng = nc.dram_tensor("lng", [128, D], F32, kind="ExternalInput").ap()
    lnb = nc.dram_tensor("lnb", [128, D], F32, kind="ExternalInput").ap()
    ident = nc.dram_tensor("ident", [128, 128], BF16, kind="ExternalInput").ap()
    hres = nc.dram_tensor("hres", [NT, D], F32, kind="ExternalOutput").ap() if do_ln else None
    qT = nc.dram_tensor("qT", [16, 128, NT], BF16, kind="ExternalOutput").ap()
    kT = nc.dram_tensor("kT", [16, 128, NT], BF16, kind="ExternalOutput").ap()
    v = nc.dram_tensor("v", [NT, D], BF16, kind="ExternalOutput").ap()
    phase_A(nc, xin, w_in, lng, lnb, ident, hres, qT, kT, v, NT, do_ln)
    return nc


def build_B(n_sb, n_df, lam_init):
    nc = bass.Bass("TRN2", target_bir_lowering=False)
    nu = n_sb + 2 * n_df
    qT = nc.dram_tensor("qT", [nu, 128, S], BF16, kind="ExternalInput").ap()
    kT = nc.dram_tensor("kT", [nu, 128, S], BF16, kind="ExternalInput").ap()
    v = nc.dram_tensor("v", [S, n_sb * 128 + n_df * 256], BF16, kind="ExternalInput").ap()
    cst = {}
    for name, shape, dt in (_b_specs(n_df) if n_df else _B_CONST_SPECS):
        cst[name] = nc.dram_tensor(name, shape, dt, kind="ExternalInput").ap()
    mT = nc.dram_tensor("mT", [nu, 128, S], BF16, kind="ExternalOutput").ap()
    phase_B(nc, qT, kT, v, cst, mT, n_sb, n_df, lam_init)
    return nc


def build_C(NT):
    nc = bass.Bass("TRN2", target_bir_lowering=False)
    mT = nc.dram_tensor("mT", [16, 128, NT], BF16, kind="ExternalInput").ap()
    hres = nc.dram_tensor("hres", [NT, D], F32, kind="ExternalInput").ap()
    w_out = nc.dram_tensor("w_out", [D, D], F32, kind="ExternalInput").ap()
    w_up = nc.dram_tensor("w_up", [D, DFF], F32, kind="ExternalInput").ap()
    w_down = nc.dram_tensor("w_down", [DFF, D], F32, kind="ExternalInput").ap()
    gb = {k: nc.dram_tensor(k, [128, D], F32, kind="ExternalInput").ap() for k in ("g1", "b1", "g2", "b2")}
    ident = nc.dram_tensor("ident", [128, 128], BF16, kind="ExternalInput").ap()
    hout = nc.dram_tensor("hout", [NT, D], F32, kind="ExternalOutput").ap()
    phase_C(nc, mT, hres, w_out, gb["g1"], gb["b1"], w_up, w_down, gb["g2"], gb["b2"], ident, hout, NT)
    return nc


def _lam_init(l):
    return 0.8 - 0.6 * math.exp(-0.3 * l)


def kernel(**inputs):
    inputs = {k: np.asarray(v) for k, v in inputs.items()}
    NC = 8
    NT = NB * S // NC
    cores = list(range(NC))
    cb = _consts()
    x = inputs["x"].reshape(NB * S, D)
    h = None
    for l in range(DEPTH):
        ncA = build_A(NT, do_ln=(l == 0))
        src = x if l == 0 else h
        in_maps = []
        for c in cores:
            in_maps.append({"xin": np.ascontiguousarray(src[c * NT:(c + 1) * NT]), "w_in": inputs["w_in"][l],
                            "lng": _bcast(inputs["ln0_g"]), "lnb": _bcast(inputs["ln0_b"]), "ident": cb["ident"]})
        resA = run_bass_kernel_spmd(ncA, in_maps, core_ids=cores).results
        if l == 0:
            h = np.concatenate([resA[c]["hres"] for c in cores], 0)
        ncB = build_B(4, 2, _lam_init(l))
        in_maps = []
        for c in cores:
            b, r = c // 2, c % 2
            units = list(range(4 * r, 4 * r + 4)) + list(range(8 + 4 * r, 8 + 4 * r + 4))
            qTb = np.concatenate([resA[2 * b]["qT"][units], resA[2 * b + 1]["qT"][units]], axis=2)
            kTb = np.concatenate([resA[2 * b]["kT"][units], resA[2 * b + 1]["kT"][units]], axis=2)
            vcols = list(range(4 * r * 128, (4 * r + 4) * 128)) + list(range(1024 + 512 * r, 1024 + 512 * r + 512))
            vb = np.concatenate([resA[2 * b]["v"][:, vcols], resA[2 * b + 1]["v"][:, vcols]], axis=0)
            m = {"qT": np.ascontiguousarray(qTb), "kT": np.ascontiguousarray(kTb), "v": np.ascontiguousarray(vb)}
            m.update({k: cb[k] for k in ("ident", "uinc", "lstr", "onesb", "ones32", "msb", "mdf")})
            m.update(_layer_consts(inputs, l, [2 * r, 2 * r + 1]))
            in_maps.append(m)
        resB = run_bass_kernel_spmd(ncB, in_maps, core_ids=cores).results
        ncC = build_C(NT)
        in_maps = []
        for c in cores:
            b, r = c // 2, c % 2
            mTb = np.empty((16, 128, NT), dtype=resB[0]["mT"].dtype)
            for rr in range(2):
                part = resB[2 * b + rr]["mT"][:, :, r * NT:(r + 1) * NT]
                mTb[4 * rr:4 * rr + 4] = part[0:4]
                mTb[8 + 4 * rr:8 + 4 * rr + 4] = part[4:8]
            in_maps.append({"mT": mTb, "hres": np.ascontiguousarray(h[c * NT:(c + 1) * NT]),
                            "w_out": inputs["w_out"][l], "w_up": inputs["w_up"][l], "w_down": inputs["w_down"][l],
                            "g1": _bcast(inputs["ln1_g"][l]), "b1": _bcast(inputs["ln1_b"][l]),
                            "g2": _bcast(inputs["ln2_g"][l]), "b2": _bcast(inputs["ln2_b"][l]), "ident": cb["ident"]})
        resC = run_bass_kernel_spmd(ncC, in_maps, core_ids=cores).results
        h = np.concatenate([resC[c]["hout"] for c in cores], 0)
    return h.reshape(NB, S, D).astype(np.float32)
```
